# Optimizing a Trainium2 kernel written in Bass

```python
import jax, jax.numpy as jnp
from jax import lax
import numpy as np

D_MODEL = 1024
BATCH = 1
SEQ = 16384
DEPTH = 2

GRID_W = 64
CTX_LEN = 256
N_Q_HEADS = 8
N_KV_HEADS = 2
HEAD_DIM = 128
ATTN_WIDTH = N_Q_HEADS * HEAD_DIM
KV_WIDTH = N_KV_HEADS * HEAD_DIM
ROPE_AXIS_DIM = HEAD_DIM // 2
ROPE_THETA = 10000.0
Q_BLOCK = 128
SGU_GROUPS = 8
SGU_WIDTH = 512
SGU_CHUNK = 128
FOURIER_GROUPS = 4
FOURIER_WIDTH = 512
N_BRANCHES = 3
FFN_HIDDEN = -(-8 * D_MODEL // (3 * 256)) * 256
Q_END = ATTN_WIDTH
K_END = Q_END + KV_WIDTH
V_END = K_END + KV_WIDTH
SGU_END = V_END + 2 * SGU_WIDTH
FOURIER_END = SGU_END + FOURIER_WIDTH
IN_WIDTH = FOURIER_END + N_BRANCHES * D_MODEL
DEEPNORM_ALPHA = (2 * DEPTH) ** 0.25
DEEPNORM_BETA = (8 * DEPTH) ** -0.25
LN_EPS = 1e-6
RMS_EPS = 1e-6

kernel_name = 'hybrid_gated_attn_gmlp_fourier_dit_block'


def layer_norm(x, g=None, b=None):
    xf = x.astype(jnp.float32)
    mu = jnp.mean(xf, axis=-1, keepdims=True)
    var = jnp.mean(jnp.square(xf - mu), axis=-1, keepdims=True)
    y = (xf - mu) * lax.rsqrt(var + LN_EPS)
    if g is not None:
        y = y * g.astype(jnp.float32) + b.astype(jnp.float32)
    return y.astype(x.dtype)


def rms_norm(x, g):
    xf = x.astype(jnp.float32)
    y = xf * lax.rsqrt(jnp.mean(jnp.square(xf), axis=-1, keepdims=True) + RMS_EPS)
    return (y * g.astype(jnp.float32)).astype(x.dtype)


def modulate(x, shift, scale):
    return layer_norm(x) * (1 + scale) + shift


def axial_rope(n_tokens):
    rows = n_tokens // GRID_W
    pos_r = jnp.repeat(jnp.arange(rows, dtype=jnp.float32), GRID_W)
    pos_c = jnp.tile(jnp.arange(GRID_W, dtype=jnp.float32), rows)
    inv = ROPE_THETA ** (-jnp.arange(0, ROPE_AXIS_DIM, 2, dtype=jnp.float32) / ROPE_AXIS_DIM)
    ang = jnp.concatenate([pos_r[:, None] * inv, pos_c[:, None] * inv], axis=-1)
    return jnp.cos(ang), jnp.sin(ang)


def apply_rope(x, cos, sin):
    xf = x.astype(jnp.float32).reshape(*x.shape[:-1], HEAD_DIM // 2, 2)
    x0, x1 = xf[..., 0], xf[..., 1]
    cc, ss = cos[None, :, None, :], sin[None, :, None, :]
    out = jnp.stack([x0 * cc - x1 * ss, x0 * ss + x1 * cc], axis=-1)
    return out.reshape(x.shape).astype(x.dtype)


def attention(q, k, v):
    B, Tq = q.shape[0], q.shape[1]
    G = N_Q_HEADS // N_KV_HEADS
    nb = Tq // Q_BLOCK
    qb = q.reshape(B, nb, Q_BLOCK, N_KV_HEADS, G, HEAD_DIM).transpose(1, 0, 2, 3, 4, 5)
    scale = HEAD_DIM ** -0.5

    def one_block(q_blk):
        s = jnp.einsum('bqhgd,bkhd->bhgqk', q_blk, k).astype(jnp.float32) * scale
        p = jax.nn.softmax(s, axis=-1).astype(v.dtype)
        return jnp.einsum('bhgqk,bkhd->bqhgd', p, v)

    o = lax.map(one_block, qb)
    return o.transpose(1, 0, 2, 3, 4, 5).reshape(B, Tq, ATTN_WIDTH)


def spatial_gating(s, ln_g, ln_b, w_sp, b_sp):
    B, T = s.shape[0], s.shape[1]
    u, v = s[..., :SGU_WIDTH], s[..., SGU_WIDTH:]
    v = layer_norm(v, ln_g, ln_b)
    n = T // SGU_CHUNK
    vg = v.reshape(B, n, SGU_CHUNK, SGU_GROUPS, SGU_WIDTH // SGU_GROUPS)
    mixed = jnp.einsum('gpq,bnqgc->bnpgc', w_sp, vg) + b_sp.T[None, None, :, :, None]
    return u * mixed.reshape(B, T, SGU_WIDTH)


def fourier_mix(f):
    B, T = f.shape[0], f.shape[1]
    fg = f.astype(jnp.float32).reshape(B, T, FOURIER_GROUPS, FOURIER_WIDTH // FOURIER_GROUPS)
    y = jnp.fft.fft2(fg, axes=(1, 3), norm='ortho').real
    return y.reshape(B, T, FOURIER_WIDTH).astype(f.dtype)


def token_mixer(h, w_in, q_gain, k_gain, sgu_ln_g, sgu_ln_b, w_sp, b_sp,
                w_br_attn, w_br_sgu, w_br_fourier, w_out, rope, ctx_kv):
    B, T, _ = h.shape
    z = h @ w_in
    q = rms_norm(z[..., :Q_END].reshape(B, T, N_Q_HEADS, HEAD_DIM), q_gain)
    k = rms_norm(z[..., Q_END:K_END].reshape(B, T, N_KV_HEADS, HEAD_DIM), k_gain)
    v = z[..., K_END:V_END].reshape(B, T, N_KV_HEADS, HEAD_DIM)
    s = jax.nn.gelu(z[..., V_END:SGU_END])
    f = z[..., SGU_END:FOURIER_END]
    gates = jax.nn.sigmoid(z[..., FOURIER_END:].astype(jnp.float32)).astype(h.dtype)
    gates = gates.reshape(B, T, N_BRANCHES, D_MODEL)
    if rope is not None:
        q = apply_rope(q, rope[0], rope[1])
        k = apply_rope(k, rope[0], rope[1])
    if ctx_kv is not None:
        k_all = jnp.concatenate([k, ctx_kv[0]], axis=1)
        v_all = jnp.concatenate([v, ctx_kv[1]], axis=1)
    else:
        k_all, v_all = k, v
    a = attention(q, k_all, v_all)
    g_out = spatial_gating(s, sgu_ln_g, sgu_ln_b, w_sp, b_sp)
    f_out = fourier_mix(f)
    merged = (gates[:, :, 0] * (a @ w_br_attn)
              + gates[:, :, 1] * (g_out @ w_br_sgu)
              + gates[:, :, 2] * (f_out @ w_br_fourier))
    return merged @ w_out, (k, v)


def context_kv(h, w_in, k_gain):
    B, T, _ = h.shape
    kv = h @ w_in[:, Q_END:V_END]
    k = rms_norm(kv[..., :KV_WIDTH].reshape(B, T, N_KV_HEADS, HEAD_DIM), k_gain)
    v = kv[..., KV_WIDTH:].reshape(B, T, N_KV_HEADS, HEAD_DIM)
    return k, v


def swiglu(h, w_up, w_down):
    gu = h @ w_up
    return (jax.nn.silu(gu[..., :FFN_HIDDEN]) * gu[..., FFN_HIDDEN:]) @ w_down


def setup_inputs(seed: int = 0) -> dict:
    key = jax.random.key(seed)
    ks = jax.random.split(key, 24)
    f32 = jnp.float32
    L, D = DEPTH, D_MODEL

    def nrm(k, shape, scale):
        return jax.random.normal(k, shape, f32) * scale

    return {
        'x': nrm(ks[0], (BATCH, SEQ, D), 1.0),
        'c': nrm(ks[1], (BATCH, D), 1.0),
        'ctx': nrm(ks[2], (BATCH, CTX_LEN, D), 1.0),
        'c_ctx': nrm(ks[3], (D,), 1.0),
        'w_ada': nrm(ks[4], (L, D, 6 * D), D ** -0.5),
        'b_ada': nrm(ks[5], (L, 6 * D), 0.01),
        'w_in': nrm(ks[6], (L, D, IN_WIDTH), D ** -0.5),
        'q_gain': 1.0 + nrm(ks[7], (L, HEAD_DIM), 0.02),
        'k_gain': 1.0 + nrm(ks[8], (L, HEAD_DIM), 0.02),
        'sgu_ln_g': 1.0 + nrm(ks[9], (L, SGU_WIDTH), 0.02),
        'sgu_ln_b': nrm(ks[10], (L, SGU_WIDTH), 0.02),
        'w_spatial': nrm(ks[11], (L, SGU_GROUPS, SGU_CHUNK, SGU_CHUNK), SGU_CHUNK ** -0.5),
        'b_spatial': 1.0 + nrm(ks[12], (L, SGU_GROUPS, SGU_CHUNK), 0.02),
        'w_br_attn': nrm(ks[13], (L, ATTN_WIDTH, D), ATTN_WIDTH ** -0.5),
        'w_br_sgu': nrm(ks[14], (L, SGU_WIDTH, D), SGU_WIDTH ** -0.5),
        'w_br_fourier': nrm(ks[15], (L, FOURIER_WIDTH, D), FOURIER_WIDTH ** -0.5),
        'w_out': nrm(ks[16], (L, D, D), D ** -0.5 * DEEPNORM_BETA),
        'ln1_g': 1.0 + nrm(ks[17], (L, D), 0.02),
        'ln1_b': nrm(ks[18], (L, D), 0.02),
        'w_up': nrm(ks[19], (L, D, 2 * FFN_HIDDEN), D ** -0.5),
        'w_down': nrm(ks[20], (L, FFN_HIDDEN, D), FFN_HIDDEN ** -0.5 * DEEPNORM_BETA),
        'ln2_g': 1.0 + nrm(ks[21], (L, D), 0.02),
        'ln2_b': nrm(ks[22], (L, D), 0.02),
    }


def reference(x, c, ctx, c_ctx, w_ada, b_ada, w_in, q_gain, k_gain, sgu_ln_g, sgu_ln_b,
              w_spatial, b_spatial, w_br_attn, w_br_sgu, w_br_fourier, w_out,
              ln1_g, ln1_b, w_up, w_down, ln2_g, ln2_b):
    rope = axial_rope(x.shape[1])
    alpha = DEEPNORM_ALPHA
    for l in range(DEPTH):
        last = l == DEPTH - 1
        mod_x = (jax.nn.silu(c) @ w_ada[l] + b_ada[l])[:, None, :]
        mod_c = (jax.nn.silu(c_ctx) @ w_ada[l] + b_ada[l])[None, None, :]
        shift1, scale1, gate1, shift2, scale2, gate2 = jnp.split(mod_x, 6, axis=-1)
        c_shift1, c_scale1, c_gate1, c_shift2, c_scale2, c_gate2 = jnp.split(mod_c, 6, axis=-1)
        mixer_w = (w_in[l], q_gain[l], k_gain[l], sgu_ln_g[l], sgu_ln_b[l], w_spatial[l],
                   b_spatial[l], w_br_attn[l], w_br_sgu[l], w_br_fourier[l], w_out[l])

        hc = modulate(ctx, c_shift1, c_scale1)
        if last:
            ctx_kv = context_kv(hc, w_in[l], k_gain[l])
        else:
            mix_c, ctx_kv = token_mixer(hc, *mixer_w, None, None)

        hx = modulate(x, shift1, scale1)
        mix_x, _ = token_mixer(hx, *mixer_w, rope, ctx_kv)
        x = layer_norm(alpha * x + gate1 * mix_x, ln1_g[l], ln1_b[l])
        ffn_x = swiglu(modulate(x, shift2, scale2), w_up[l], w_down[l])
        x = layer_norm(alpha * x + gate2 * ffn_x, ln2_g[l], ln2_b[l])

        if not last:
            ctx = layer_norm(alpha * ctx + c_gate1 * mix_c, ln1_g[l], ln1_b[l])
            ffn_c = swiglu(modulate(ctx, c_shift2, c_scale2), w_up[l], w_down[l])
            ctx = layer_norm(alpha * ctx + c_gate2 * ffn_c, ln2_g[l], ln2_b[l])
    return x
```

```python
import math
import numpy as np
import concourse.bass as bass
import concourse.mybir as mybir
from concourse.bass import AP
from concourse.bass_utils import run_bass_kernel_spmd

F32 = mybir.dt.float32
BF16 = mybir.dt.bfloat16
AF = mybir.ActivationFunctionType
ALU = mybir.AluOpType
AX = mybir.AxisListType
ENGS = ("tensor", "vector", "scalar", "gpsimd", "sync")
NDMA = 24

NCORES = 8
D = 1024
SEQ = 16384
CTX = 256
DEPTH = 2
TOK = SEQ // NCORES
NH, NKV, HD = 8, 2, 128
FFN = 2816
ALPHA = (2 * DEPTH) ** 0.25
LN_EPS = 1e-6
RMS_EPS = 1e-6
IN_W = 6144


class Buf:
    __slots__ = ("w", "r")

    def __init__(self):
        self.w = None
        self.r = []


class _Rec:
    def __init__(self):
        self.call = None

    def __getattr__(self, name):
        def f(*a, **kw):
            self.call = (name, a, kw)
            return self
        return f


class Prog:
    def __init__(self, nc):
        self.nc = nc
        self.q = {e: [] for e in ENGS}
        self.cnt = {e: 0 for e in ENGS}
        self.seen = {e: {} for e in ENGS}
        self.dslot = {e: 0 for e in ENGS}
        self.dval = {}
        self.stack = []
        self.uid = 0

    def _name(self, n):
        self.uid += 1
        return f"{n}_{self.uid}"

    def sb(self, name, shape, dt):
        g = self.nc.sbuf_tensor(self._name(name), list(shape), dt)
        t = g.__enter__()
        self.stack.append(g)
        return t

    def ps(self, name, shape, dt=F32):
        g = self.nc.psum_tensor(self._name(name), list(shape), dt)
        t = g.__enter__()
        self.stack.append(g)
        return t

    def mark(self):
        return len(self.stack)

    def release(self, mark):
        self.barrier()
        while len(self.stack) > mark:
            self.stack.pop().__exit__(None, None, None)

    def barrier(self):
        allk = [(("e", e), self.cnt[e]) for e in ENGS if self.cnt[e]]
        allk += list(self.dval.items())
        for e in ENGS:
            waits = []
            seen = self.seen[e]
            for s, v in allk:
                if seen.get(s, 0) < v:
                    seen[s] = v
                    waits.append((s, v))
            if waits:
                self.q[e].append((waits, None, None))

    def _deps(self, eng, reads, writes):
        deps = {}

        def add(k):
            if k is None:
                return
            s, v = k
            if deps.get(s, 0) < v:
                deps[s] = v
        for b in reads:
            add(b.w)
        for b in writes:
            add(b.w)
            for r in b.r:
                add(r)
        out = []
        seen = self.seen[eng]
        for s, v in deps.items():
            if eng == "tensor" and s == ("e", "tensor"):
                continue
            if seen.get(s, 0) >= v:
                continue
            seen[s] = v
            out.append((s, v))
        return out

    def _mark(self, key, reads, writes):
        for b in reads:
            b.r.append(key)
            if len(b.r) > 64:
                b.r = b.r[-64:]
        for b in writes:
            b.w = key
            b.r = []

    def op(self, eng, fn, reads=(), writes=()):
        rec = _Rec()
        fn(rec)
        call = rec.call

        def fn(e, call=call):
            return getattr(e, call[0])(*call[1], **call[2])
        waits = self._deps(eng, reads, writes)
        self.cnt[eng] += 1
        key = (("e", eng), self.cnt[eng])
        self.q[eng].append((waits, fn, key))
        self._mark(key, reads, writes)

    def dma(self, eng, out, in_, reads=(), writes=(), **kw):
        slot = self.dslot[eng] % NDMA
        self.dslot[eng] += 1
        s = ("d", eng, slot)
        prev = self.dval.get(s, 0)
        waits = self._deps(eng, reads, writes)
        seen = self.seen[eng]
        if prev and seen.get(s, 0) < prev:
            seen[s] = prev
            waits.append((s, prev))
        val = prev + 16
        self.dval[s] = val
        key = (s, val)

        def fn(e, out=out, in_=in_, kw=kw):
            return e.dma_start(out=out, in_=in_, **kw)
        self.q[eng].append((waits, fn, key))
        self._mark(key, reads, writes)

    def finish(self):
        nc = self.nc
        fin = []
        for s, v in self.dval.items():
            if self.seen["sync"].get(s, 0) < v:
                fin.append((s, v))
        for e in ENGS:
            if e != "sync" and self.cnt[e]:
                fin.append((("e", e), self.cnt[e]))
        sems = {}
        guards = []

        def getsem(s):
            if s not in sems:
                g = nc.semaphore("s_" + "_".join(str(x) for x in s))
                sems[s] = g.__enter__()
                guards.append(g)
        for e in ENGS:
            for waits, fn, key in self.q[e]:
                for s, v in waits:
                    getsem(s)
                if key is not None:
                    getsem(key[0])
        for s, v in fin:
            getsem(s)
        q = self.q

        def body(ename):
            def run(eh):
                for waits, fn, key in q[ename]:
                    for s, v in waits:
                        eh.wait_ge(sems[s], v)
                    if fn is not None:
                        fn(eh).then_inc(sems[key[0]], 16 if key[0][0] == "d" else 1)
                if ename == "sync":
                    for s, v in fin:
                        eh.wait_ge(sems[s], v)
            return run
        with nc.Block() as block:
            block.sync(body("sync"))
            block.tensor(body("tensor"))
            block.vector(body("vector"))
            block.scalar(body("scalar"))
            block.gpsimd(body("gpsimd"))
        for g in reversed(guards):
            g.__exit__(None, None, None)
        while self.stack:
            self.stack.pop().__exit__(None, None, None)


def bc_mid(ap2, n):
    return AP(ap2.tensor, ap2.offset, [list(ap2.ap[0]), [0, n]] + [list(x) for x in ap2.ap[1:]])


def bc_last(ap2, n):
    return AP(ap2.tensor, ap2.offset, [list(x) for x in ap2.ap] + [[0, n]])


class Ctx:
    pass


def dram(nc, name, shape, dt, kind):
    return nc.dram_tensor(name, list(shape), dt, kind=kind).ap()


def ln_affine(P, C, src, Bsrc, nb, A, BA, Bv, out, Bout):
    xb, x2b, pmean, pm2 = C.ln_xb, C.ln_x2b, C.ps_a, C.ps_b
    P.op("scalar", lambda e: e.activation(out=xb[:, :, 0:nb], in_=src[:, :, 0:nb], func=AF.Copy), [Bsrc], [C.B_xb])
    P.op("scalar", lambda e: e.activation(out=x2b[:, :, 0:nb], in_=src[:, :, 0:nb], func=AF.Square), [Bsrc], [C.B_x2b])
    for k in range(8):
        P.op("tensor", lambda e, k=k: e.matmul(pmean[:, 0:nb], lhsT=C.onesD[:], rhs=xb[:, k, 0:nb], start=(k == 0), stop=(k == 7)),
             [C.B_xb, C.B_const], [C.B_psa])
    for k in range(8):
        P.op("tensor", lambda e, k=k: e.matmul(pm2[:, 0:nb], lhsT=C.onesD[:], rhs=x2b[:, k, 0:nb], start=(k == 0), stop=(k == 7)),
             [C.B_x2b, C.B_const], [C.B_psb])
    mean, rstd, tmp = C.ln_mean, C.ln_rstd, C.ln_tmp
    P.op("scalar", lambda e: e.activation(out=mean[:, 0:nb], in_=pmean[:, 0:nb], func=AF.Copy), [C.B_psa], [C.B_mean])
    P.op("vector", lambda e: e.tensor_tensor(out=rstd[:, 0:nb], in0=mean[:, 0:nb], in1=mean[:, 0:nb], op=ALU.mult), [C.B_mean], [C.B_rstd])
    P.op("vector", lambda e: e.tensor_tensor(out=rstd[:, 0:nb], in0=pm2[:, 0:nb], in1=rstd[:, 0:nb], op=ALU.subtract), [C.B_psb, C.B_rstd], [C.B_rstd])
    P.op("scalar", lambda e: e.activation(out=rstd[:, 0:nb], in_=rstd[:, 0:nb], func=AF.Sqrt, bias=C.eps_ln[:], scale=1.0), [C.B_rstd, C.B_const], [C.B_rstd])
    P.op("vector", lambda e: e.reciprocal(out=rstd[:, 0:nb], in_=rstd[:, 0:nb]), [C.B_rstd], [C.B_rstd])
    for k in range(8):
        P.op("vector", lambda e, k=k: e.tensor_tensor(out=tmp[:, k, 0:nb], in0=src[:, k, 0:nb], in1=mean[:, 0:nb], op=ALU.subtract),
             [Bsrc, C.B_mean], [C.B_tmp[k]])
        P.op("vector", lambda e, k=k: e.tensor_tensor(out=tmp[:, k, 0:nb], in0=tmp[:, k, 0:nb], in1=rstd[:, 0:nb], op=ALU.mult),
             [C.B_rstd, C.B_tmp[k]], [C.B_tmp[k]])
        P.op("scalar", lambda e, k=k: e.activation(out=out[:, k, 0:nb], in_=tmp[:, k, 0:nb], func=AF.Identity,
                                                  scale=A[:, k:k + 1], bias=Bv[:, k:k + 1]),
             [C.B_tmp[k], BA], [Bout])


def ln_alloc(P, C, nbmax):
    C.ln_xb = P.sb("ln_xb", [128, 8, nbmax], BF16)
    C.ln_x2b = P.sb("ln_x2b", [128, 8, nbmax], BF16)
    C.ln_mean = P.sb("ln_mean", [128, nbmax], F32)
    C.ln_rstd = P.sb("ln_rstd", [128, nbmax], F32)
    C.ln_tmp = P.sb("ln_tmp", [128, 8, nbmax], F32)
    C.B_xb, C.B_x2b, C.B_mean, C.B_rstd = Buf(), Buf(), Buf(), Buf()
    C.B_tmp = [Buf() for _ in range(8)]


def consts_alloc(P, C, d_ident, d_ones):
    C.B_const = Buf()
    C.ident = P.sb("ident", [128, 128], BF16)
    C.onesD = P.sb("onesD", [128, 128], BF16)
    C.ones1 = P.sb("ones1", [128, 128], BF16)
    C.eps_ln = P.sb("eps_ln", [128, 1], F32)
    P.dma("gpsimd", C.ident[:], d_ident, writes=[C.B_const])
    P.dma("gpsimd", C.onesD[:], d_ones[0], writes=[C.B_const])
    P.dma("gpsimd", C.ones1[:], d_ones[1], writes=[C.B_const])
    P.op("vector", lambda e: e.memset(C.eps_ln[:], LN_EPS), [], [C.B_const])


TM_W = 2560


def build_pre(segs):
    nc = bass.Bass("TRN2", target_bir_lowering=False)
    P = Prog(nc)
    C = Ctx()
    d = {}
    d["ident"] = dram(nc, "ident", [128, 128], F32, "ExternalInput")
    d["ones"] = dram(nc, "ones", [2, 128, 128], F32, "ExternalInput")
    d["scT"] = dram(nc, "scT", [128, 8, 2], F32, "ExternalInput")
    d["w_ada"] = dram(nc, "w_ada", [D, 6 * D], F32, "ExternalInput")
    d["b_adaT"] = dram(nc, "b_adaT", [128, 48], F32, "ExternalInput")
    d["w_tm"] = dram(nc, "w_tm", [D, TM_W], F32, "ExternalInput")
    d["gain"] = dram(nc, "gain", [128, 1280], F32, "ExternalInput")
    d["sgu_g"] = dram(nc, "sgu_g", [128, 512], F32, "ExternalInput")
    d["sgu_b"] = dram(nc, "sgu_b", [128, 512], F32, "ExternalInput")
    d["wspT"] = dram(nc, "wspT", [128, 8, 128], F32, "ExternalInput")
    d["bspT"] = dram(nc, "bspT", [128, 4, 128], F32, "ExternalInput")
    d["modT"] = dram(nc, "modT", [128, 48, 2], F32, "ExternalOutput")
    for (name, ntok, nb, col, rope) in segs:
        nt = ntok // 128
        d["xT_" + name] = dram(nc, "xT_" + name, [128, 8, ntok], F32, "ExternalInput")
        if rope:
            d["cos_" + name] = dram(nc, "cos_" + name, [128, nt, 64], F32, "ExternalInput")
            d["sin_" + name] = dram(nc, "sin_" + name, [128, nt, 64], F32, "ExternalInput")
        d["hT_" + name] = dram(nc, "hT_" + name, [128, 8, ntok], BF16, "ExternalOutput")
        d["qT_" + name] = dram(nc, "qT_" + name, [128, 8, ntok], BF16, "ExternalOutput")
        d["kT_" + name] = dram(nc, "kT_" + name, [128, 2, ntok], BF16, "ExternalOutput")
        d["v_" + name] = dram(nc, "v_" + name, [128, nt, 256], BF16, "ExternalOutput")
        d["smT_" + name] = dram(nc, "smT_" + name, [128, 4, ntok], BF16, "ExternalOutput")
        d["f_" + name] = dram(nc, "f_" + name, [4, ntok, 128], BF16, "ExternalOutput")

    consts_alloc(P, C, d["ident"], d["ones"])
    pz = [P.ps(f"pz{i}", [128, 512]) for i in range(5)]
    Bpz = [Buf() for _ in range(5)]
    C.ps_a, C.ps_b = pz[0], pz[1]
    C.B_psa, C.B_psb = Bpz[0], Bpz[1]
    pTq = P.ps("pTq", [128, 1024], BF16); BpTq = Buf()
    pTk = P.ps("pTk", [128, 1024], BF16); BpTk = Buf()
    pM = P.ps("pM", [128, 512]); BpM = Buf()

    mod = P.sb("mod", [128, 48, 2], F32); Bmod = Buf()
    mod1 = P.sb("mod1", [128, 8, 2], F32)
    m0 = P.mark()
    sc = P.sb("sc", [128, 8, 2], F32); Bsc = Buf()
    badaT = P.sb("badaT", [128, 48], F32); Bbada = Buf()
    P.dma("sync", sc[:], d["scT"], writes=[Bsc])
    P.dma("sync", badaT[:], d["b_adaT"], writes=[Bbada])
    P.op("scalar", lambda e: e.activation(out=sc[:], in_=sc[:], func=AF.Silu), [Bsc], [Bsc])
    wa = [P.sb(f"wa{i}", [128, 8, 768], F32) for i in range(2)]
    Bwa = [Buf(), Buf()]
    pmod = pM
    for s in range(8):
        w = wa[s % 2]
        for k in range(8):
            P.dma("sync" if k % 2 == 0 else "gpsimd", w[:, k, :], d["w_ada"][k * 128:(k + 1) * 128, s * 768:(s + 1) * 768], writes=[Bwa[s % 2]])
        for j in range(6):
            ch = s * 6 + j
            for k in range(8):
                P.op("tensor", lambda e, w=w, j=j, k=k, ch=ch: e.matmul(pmod[:, 2 * ch:2 * ch + 2], lhsT=w[:, k, j * 128:(j + 1) * 128], rhs=sc[:, k, :],
                                                                       start=(k == 0), stop=(k == 7)), [Bwa[s % 2], Bsc], [BpM])
    pm3 = pmod[:, 0:96].rearrange("p (c t) -> p c t", t=2)
    P.op("vector", lambda e: e.tensor_tensor(out=mod[:], in0=pm3, in1=bc_last(badaT[:], 2), op=ALU.add), [BpM, Bbada], [Bmod])
    P.op("vector", lambda e: e.tensor_scalar(out=mod1[:], in0=mod[:, 8:16, :], scalar1=1.0, scalar2=None, op0=ALU.add), [Bmod], [Bmod])
    P.dma("sync", d["modT"], mod[:], reads=[Bmod])
    P.release(m0)

    Bw = Buf()
    wtm = P.sb("wtm", [128, 8, TM_W], BF16)
    for k in range(8):
        P.dma("gpsimd", wtm[:, k, :], d["w_tm"][k * 128:(k + 1) * 128, :], writes=[Bw])
    gain = P.sb("gain", [128, 1280], F32)
    sgug = P.sb("sgug", [128, 512], F32)
    sgub = P.sb("sgub", [128, 512], F32)
    wsp = P.sb("wsp", [128, 8, 128], BF16)
    bsp = P.sb("bsp", [128, 4, 128], F32)
    P.dma("sync", gain[:], d["gain"], writes=[Bw])
    P.dma("sync", sgug[:], d["sgu_g"], writes=[Bw])
    P.dma("sync", sgub[:], d["sgu_b"], writes=[Bw])
    P.dma("gpsimd", wsp[:], d["wspT"], writes=[Bw])
    P.dma("sync", bsp[:], d["bspT"], writes=[Bw])
    eps_rms = P.sb("eps_rms", [128, 1], F32)
    P.op("vector", lambda e: e.memset(eps_rms[:], RMS_EPS), [], [Bw])

    NBM = max(s[2] for s in segs)
    ln_alloc(P, C, NBM)
    xs = P.sb("xs", [128, 8, NBM], F32); Bxs = Buf()
    hT = P.sb("hT", [128, 8, NBM], BF16); BhT = Buf()
    qTs = P.sb("qTs", [128, 8, NBM], BF16); BqTs = Buf()
    kTs = P.sb("kTs", [128, 2, NBM], BF16); BkTs = Buf()
    vs = P.sb("vs", [128, NBM // 128, 256], BF16); Bvs = Buf()
    smTs = P.sb("smTs", [128, 4, NBM], BF16); BsmTs = Buf()
    fs = P.sb("fs", [128, NBM // 128, 512], BF16); Bfs = Buf()
    sq = P.sb("sq", [128, 1280], F32); Bsq = Buf()
    ssq = P.sb("ssq", [128, 10], F32); Bssq = Buf()
    tq = P.sb("tq", [128, 1280], F32); Btq = Buf()
    r1 = P.sb("r1", [128, 640], F32); Br1 = Buf()
    r2 = P.sb("r2", [128, 640], F32); Br2 = Buf()
    rq = P.sb("rq", [128, 1280], BF16); Brq = Buf()
    cs = P.sb("cs", [128, 2, 64], F32); Bcs = Buf()
    sg = P.sb("sg", [128, 512], F32); Bsg = Buf()
    st = P.sb("st", [128, 6], F32); Bst = Buf()
    mv = P.sb("mv", [128, 2], F32); Bmv = Buf()
    vln = P.sb("vln", [128, 512], BF16); Bvln = Buf()

    for (name, ntok, nb, col, rope) in segs:
        A = mod1[:, :, col]
        Bv = mod[:, 0:8, col]
        for b in range(ntok // nb):
            t0 = b * nb
            P.dma("sync", xs[:, :, 0:nb], d["xT_" + name][:, :, t0:t0 + nb], writes=[Bxs])
            ln_affine(P, C, xs, Bxs, nb, A, Bmod, Bv, hT, BhT)
            P.dma("sync", d["hT_" + name][:, :, t0:t0 + nb], hT[:, :, 0:nb], reads=[BhT])
            for ti in range(nb // 128):
                tg = (t0 // 128) + ti
                tok = slice(ti * 128, (ti + 1) * 128)
                for c5 in range(5):
                    for k in range(8):
                        P.op("tensor", lambda e, c5=c5, k=k, tok=tok: e.matmul(pz[c5][:], lhsT=hT[:, k, tok], rhs=wtm[:, k, c5 * 512:(c5 + 1) * 512],
                                                                             start=(k == 0), stop=(k == 7)), [BhT, Bw], [Bpz[c5]])
                P.op("scalar", lambda e: e.activation(out=sq[:, 0:512], in_=pz[0][:], func=AF.Square), [Bpz[0]], [Bsq])
                P.op("scalar", lambda e: e.activation(out=sq[:, 512:1024], in_=pz[1][:], func=AF.Square), [Bpz[1]], [Bsq])
                P.op("scalar", lambda e: e.activation(out=sq[:, 1024:1280], in_=pz[2][:, 0:256], func=AF.Square), [Bpz[2]], [Bsq])
                P.op("vector", lambda e: e.tensor_reduce(out=ssq[:], in_=sq[:].rearrange("p (h d) -> p h d", d=128), axis=AX.X, op=ALU.add), [Bsq], [Bssq])
                P.op("scalar", lambda e: e.activation(out=ssq[:], in_=ssq[:], func=AF.Sqrt, bias=eps_rms[:], scale=1.0 / 128), [Bssq, Bw], [Bssq])
                P.op("vector", lambda e: e.reciprocal(out=ssq[:], in_=ssq[:]), [Bssq], [Bssq])
                P.op("vector", lambda e: e.tensor_tensor(out=tq[:, 0:512], in0=pz[0][:], in1=gain[:, 0:512], op=ALU.mult), [Bpz[0], Bw], [Btq])
                P.op("vector", lambda e: e.tensor_tensor(out=tq[:, 512:1024], in0=pz[1][:], in1=gain[:, 512:1024], op=ALU.mult), [Bpz[1], Bw], [Btq])
                P.op("vector", lambda e: e.tensor_tensor(out=tq[:, 1024:1280], in0=pz[2][:, 0:256], in1=gain[:, 1024:1280], op=ALU.mult), [Bpz[2], Bw], [Btq])
                tq3 = tq[:].rearrange("p (h d) -> p h d", d=128)
                if rope:
                    P.op("vector", lambda e, tq3=tq3: e.tensor_tensor(out=tq3, in0=tq3, in1=bc_last(ssq[:], 128), op=ALU.mult), [Btq, Bssq], [Btq])
                    P.dma("sync", cs[:, 0, :], d["cos_" + name][:, tg, :], writes=[Bcs])
                    P.dma("sync", cs[:, 1, :], d["sin_" + name][:, tg, :], writes=[Bcs])
                    tq4 = tq[:].rearrange("p (h d two) -> p h d two", d=64, two=2)
                    ev, od = tq4[:, :, :, 0], tq4[:, :, :, 1]
                    rq4 = rq[:].rearrange("p (h d two) -> p h d two", d=64, two=2)
                    r13 = r1[:].rearrange("p (h d) -> p h d", d=64)
                    r23 = r2[:].rearrange("p (h d) -> p h d", d=64)
                    cosb, sinb = bc_mid(cs[:, 0, :], 10), bc_mid(cs[:, 1, :], 10)
                    P.op("vector", lambda e, ev=ev, r13=r13, cosb=cosb: e.tensor_tensor(out=r13, in0=ev, in1=cosb, op=ALU.mult), [Btq, Bcs], [Br1])
                    P.op("vector", lambda e, od=od, r23=r23, sinb=sinb: e.tensor_tensor(out=r23, in0=od, in1=sinb, op=ALU.mult), [Btq, Bcs], [Br2])
                    P.op("vector", lambda e, rq4=rq4, r13=r13, r23=r23: e.tensor_tensor(out=rq4[:, :, :, 0], in0=r13, in1=r23, op=ALU.subtract), [Br1, Br2], [Brq])
                    P.op("vector", lambda e, ev=ev, r13=r13, sinb=sinb: e.tensor_tensor(out=r13, in0=ev, in1=sinb, op=ALU.mult), [Btq, Bcs, Brq], [Br1])
                    P.op("vector", lambda e, od=od, r23=r23, cosb=cosb: e.tensor_tensor(out=r23, in0=od, in1=cosb, op=ALU.mult), [Btq, Bcs, Brq], [Br2])
                    P.op("vector", lambda e, rq4=rq4, r13=r13, r23=r23: e.tensor_tensor(out=rq4[:, :, :, 1], in0=r13, in1=r23, op=ALU.add), [Br1, Br2], [Brq])
                else:
                    rq3 = rq[:].rearrange("p (h d) -> p h d", d=128)
                    P.op("vector", lambda e, tq3=tq3, rq3=rq3: e.tensor_tensor(out=rq3, in0=tq3, in1=bc_last(ssq[:], 128), op=ALU.mult), [Btq, Bssq], [Brq])
                for h in range(8):
                    P.op("tensor", lambda e, h=h: e.transpose(pTq[:, h * 128:(h + 1) * 128], rq[:, h * 128:(h + 1) * 128], C.ident[:]), [Brq, C.B_const], [BpTq])
                for h in range(2):
                    P.op("tensor", lambda e, h=h: e.transpose(pTk[:, h * 128:(h + 1) * 128], rq[:, (8 + h) * 128:(9 + h) * 128], C.ident[:]), [Brq, C.B_const], [BpTk])
                P.op("scalar", lambda e, tok=tok: e.activation(out=qTs[:, :, tok], in_=pTq[:].rearrange("p (h t) -> p h t", t=128), func=AF.Copy), [BpTq], [BqTs])
                P.op("scalar", lambda e, tok=tok: e.activation(out=kTs[:, :, tok], in_=pTk[:, 0:256].rearrange("p (h t) -> p h t", t=128), func=AF.Copy), [BpTk], [BkTs])
                P.op("scalar", lambda e, ti=ti: e.activation(out=vs[:, ti, :], in_=pz[2][:, 256:512], func=AF.Copy), [Bpz[2]], [Bvs])
                P.op("scalar", lambda e, ti=ti: e.activation(out=fs[:, ti, :], in_=pz[4][:], func=AF.Copy), [Bpz[4]], [Bfs])
                P.op("scalar", lambda e: e.activation(out=sg[:], in_=pz[3][:], func=AF.Gelu_apprx_tanh), [Bpz[3]], [Bsg])
                P.op("vector", lambda e: e.bn_stats(out=st[:], in_=sg[:]), [Bsg], [Bst])
                P.op("vector", lambda e: e.bn_aggr(out=mv[:], in_=st[:]), [Bst], [Bmv])
                P.op("scalar", lambda e: e.activation(out=mv[:, 1:2], in_=mv[:, 1:2], func=AF.Sqrt, bias=C.eps_ln[:], scale=1.0), [Bmv, C.B_const], [Bmv])
                P.op("vector", lambda e: e.reciprocal(out=mv[:, 1:2], in_=mv[:, 1:2]), [Bmv], [Bmv])
                P.op("vector", lambda e: e.tensor_scalar(out=sg[:], in0=sg[:], scalar1=mv[:, 0:1], scalar2=mv[:, 1:2], op0=ALU.subtract, op1=ALU.mult), [Bsg, Bmv], [Bsg])
                P.op("vector", lambda e: e.tensor_tensor(out=sg[:], in0=sg[:], in1=sgug[:], op=ALU.mult), [Bsg, Bw], [Bsg])
                P.op("vector", lambda e: e.tensor_tensor(out=vln[:], in0=sg[:], in1=sgub[:], op=ALU.add), [Bsg, Bw], [Bvln])
                for g in range(8):
                    po = (g % 2) * 64
                    P.op("tensor", lambda e, g=g, po=po: e.matmul(pM[po:po + 64, (g // 2) * 128:(g // 2 + 1) * 128], lhsT=vln[:, g * 64:(g + 1) * 64], rhs=wsp[:, g, :],
                                                                  start=True, stop=True), [Bvln, Bw], [BpM])
                P.op("vector", lambda e, tok=tok: e.tensor_tensor(out=smTs[:, :, tok], in0=pM[:].rearrange("p (j t) -> p j t", t=128), in1=bsp[:], op=ALU.add), [BpM, Bw], [BsmTs])
            nt = nb // 128
            tg0 = t0 // 128
            P.dma("sync", d["qT_" + name][:, :, t0:t0 + nb], qTs[:, :, 0:nb], reads=[BqTs])
            P.dma("sync", d["kT_" + name][:, :, t0:t0 + nb], kTs[:, :, 0:nb], reads=[BkTs])
            P.dma("sync", d["v_" + name][:, tg0:tg0 + nt, :], vs[:, 0:nt, :], reads=[Bvs])
            P.dma("sync", d["smT_" + name][:, :, t0:t0 + nb], smTs[:, :, 0:nb], reads=[BsmTs])
            for g in range(4):
                P.dma("sync", d["f_" + name][g, t0:t0 + nb, :].rearrange("(t p) c -> p t c", p=128), fs[:, 0:nt, g * 128:(g + 1) * 128], reads=[Bfs])
    P.finish()
    return nc


def fm(a):
    ntok = a.shape[0]
    return np.ascontiguousarray(a.T.reshape(8, 128, ntok).transpose(1, 0, 2))


def unfm(a):
    ntok = a.shape[2]
    return np.ascontiguousarray(a.transpose(1, 0, 2).reshape(1024, ntok).T)


def vecT(v, nch):
    return np.ascontiguousarray(v.reshape(nch, 128).T)


_ROPE = None


def rope_tables():
    global _ROPE
    if _ROPE is None:
        t = np.arange(SEQ)
        pos_r = (t // 64).astype(np.float32)
        pos_c = (t % 64).astype(np.float32)
        inv = (10000.0 ** (-np.arange(0, 64, 2, dtype=np.float32) / 64)).astype(np.float32)
        ang = np.concatenate([pos_r[:, None] * inv, pos_c[:, None] * inv], axis=-1).astype(np.float32)
        _ROPE = (np.cos(ang).astype(np.float32), np.sin(ang).astype(np.float32))
    return _ROPE


_PROGS = {}


def get_prog(key, builder):
    if key not in _PROGS:
        _PROGS[key] = builder()
    return _PROGS[key]


SEG_X = ("x", TOK, 512, 0, True)
SEG_C = ("c", CTX, 256, 1, False)


def pre_inputs(l, inp, xT_cores, ctxT):
    f32 = np.float32
    common = {
        "ident": np.eye(128, dtype=f32),
        "ones": np.stack([np.full((128, 128), 1.0 / 1024, f32), np.ones((128, 128), f32)]),
        "scT": np.ascontiguousarray(np.stack([vecT(inp["c"][0], 8), vecT(inp["c_ctx"], 8)], axis=-1)),
        "w_ada": np.ascontiguousarray(inp["w_ada"][l]),
        "b_adaT": vecT(inp["b_ada"][l], 48),
        "w_tm": np.ascontiguousarray(np.concatenate([inp["w_in"][l][:, 0:1536], inp["w_in"][l][:, 2048:3072]], axis=1)),
        "gain": np.ascontiguousarray(np.broadcast_to(np.concatenate([np.tile(inp["q_gain"][l], 8), np.tile(inp["k_gain"][l], 2)])[None, :], (128, 1280))),
        "sgu_g": np.ascontiguousarray(np.broadcast_to(inp["sgu_ln_g"][l][None, :], (128, 512))),
        "sgu_b": np.ascontiguousarray(np.broadcast_to(inp["sgu_ln_b"][l][None, :], (128, 512))),
        "wspT": np.ascontiguousarray(inp["w_spatial"][l].transpose(2, 0, 1)),
        "bspT": np.ascontiguousarray(np.repeat(inp["b_spatial"][l].reshape(4, 2, 128).transpose(1, 0, 2), 64, axis=0)),
        "xT_c": ctxT,
    }
    cos, sin = rope_tables()
    maps = []
    for i in range(NCORES):
        m = dict(common)
        m["xT_x"] = xT_cores[i]
        sl = slice(i * TOK, (i + 1) * TOK)
        m["cos_x"] = np.ascontiguousarray(cos[sl].reshape(TOK // 128, 128, 64).transpose(1, 0, 2))
        m["sin_x"] = np.ascontiguousarray(sin[sl].reshape(TOK // 128, 128, 64).transpose(1, 0, 2))
        maps.append(m)
    return maps


def run_pre(l, inp, xT_cores, ctxT):
    nc = get_prog("pre", lambda: build_pre([SEG_C, SEG_X]))
    res = run_bass_kernel_spmd(nc, pre_inputs(l, inp, xT_cores, ctxT), core_ids=list(range(NCORES)))
    return res.results


FM_W = 3584
NKT = (SEQ + CTX) // 128


def stage_attn(P, C, d, segs):
    m = P.mark()
    kTs = P.sb("a_kT", [128, SEQ + CTX], BF16); BkT = Buf()
    vs = P.sb("a_v", [128, NKT, 128], BF16); Bv = Buf()
    qs = P.sb("a_q", [128, 4, TOK], BF16); Bq = Buf()
    pT = [P.sb(f"a_pT{i}", [128, 512], BF16) for i in range(3)]; BpT = [Buf() for _ in range(3)]
    rs = [P.sb(f"a_rs{i}", [128, 512], F32) for i in range(2)]; Brs = [Buf(), Buf()]
    ao = [P.sb(f"a_ao{i}", [128, 512], BF16) for i in range(2)]; Bao = [Buf(), Buf()]
    ps_s = [P.ps(f"a_ps{i}", [128, 512]) for i in range(3)]; Bps = [Buf() for _ in range(3)]
    ps_o = [P.ps(f"a_po{i}", [128, 512]) for i in range(2)]; Bpo = [Buf(), Buf()]
    ps_l = [P.ps(f"a_pl{i}", [128, 512]) for i in range(2)]; Bpl = [Buf(), Buf()]
    scale = 1.0 / math.sqrt(HD)
    it = 0
    for (name, ntok, nb, col, rope) in segs:
        kt0, nkt = (0, NKT) if name == "x" else (SEQ // 128, CTX // 128)
        for h in range(2):
            P.dma("sync", kTs[:, 0:nkt * 128], d["kT_all"][:, h, kt0 * 128:(kt0 + nkt) * 128], writes=[BkT])
            P.dma("gpsimd", vs[:, 0:nkt, :], d["v_all"][h, :, kt0:kt0 + nkt, :], writes=[Bv])
            P.dma("sync", qs[:, :, 0:ntok], d["qT_" + name][:, 4 * h:4 * h + 4, :], writes=[Bq])
            for qb in range(ntok // nb):
                q0 = qb * nb
                for hh in range(4):
                    o, l = it % 2, it % 2
                    it += 1

                    def qk(kt):
                        P.op("tensor", lambda e, kt=kt: e.matmul(ps_s[kt % 3][:, 0:nb], lhsT=kTs[:, kt * 128:(kt + 1) * 128], rhs=qs[:, hh, q0:q0 + nb],
                                                                start=True, stop=True), [BkT, Bq], [Bps[kt % 3]])
                    qk(0)
                    if nkt > 1:
                        qk(1)
                    for kt in range(nkt):
                        P.op("scalar", lambda e, kt=kt: e.activation(out=pT[kt % 3][:, 0:nb], in_=ps_s[kt % 3][:, 0:nb], func=AF.Exp, scale=scale),
                             [Bps[kt % 3]], [BpT[kt % 3]])
                        if kt + 2 < nkt:
                            qk(kt + 2)
                        P.op("tensor", lambda e, kt=kt: e.matmul(ps_o[o][:, 0:nb], lhsT=vs[:, kt, :], rhs=pT[kt % 3][:, 0:nb], start=(kt == 0), stop=(kt == nkt - 1)),
                             [Bv, BpT[kt % 3]], [Bpo[o]])
                        P.op("tensor", lambda e, kt=kt: e.matmul(ps_l[l][:, 0:nb], lhsT=C.ones1[:], rhs=pT[kt % 3][:, 0:nb], start=(kt == 0), stop=(kt == nkt - 1)),
                             [C.B_const, BpT[kt % 3]], [Bpl[l]])
                    P.op("vector", lambda e: e.reciprocal(out=rs[o][:, 0:nb], in_=ps_l[l][:, 0:nb]), [Bpl[l]], [Brs[o]])
                    P.op("vector", lambda e: e.tensor_tensor(out=ao[o][:, 0:nb], in0=ps_o[o][:, 0:nb], in1=rs[o][:, 0:nb], op=ALU.mult), [Bpo[o], Brs[o]], [Bao[o]])
                    P.dma("sync", d["AT_" + name][:, 4 * h + hh, q0:q0 + nb], ao[o][:, 0:nb], reads=[Bao[o]], writes=[C.B_AT])
    P.release(m)


def stage_four(P, C, d, segs):
    m = P.mark()
    pG = [P.ps(f"f_pG{i}", [128, 512]) for i in range(2)]; BpG = [Buf(), Buf()]
    pX = [P.ps(f"f_pX{i}", [128, 512]) for i in range(2)]; BpX = [Buf(), Buf()]
    fT1 = P.sb("f_fT1", [128, 128, 128], BF16); BfT1 = Buf()
    Gs = P.sb("f_Gs", [128, 128, 256], BF16); BGs = Buf()
    XTs = P.sb("f_XTs", [128, 2, TOK], BF16); BXTs = Buf()
    for (name, ntok, nb, col, rope) in segs:
        T1, nk2 = (128, 16) if name == "x" else (2, 128)
        W1 = 2 * T1
        m2 = P.mark()
        Bt = Buf()
        cs1 = P.sb("f_cs1", [T1, W1], BF16)
        tab = P.sb("f_tab", [128, T1, 2, 2 * nk2], BF16)
        P.dma("gpsimd", cs1[:], d["cs1_" + name], writes=[Bt])
        P.dma("gpsimd", tab[:], d["tab_" + name], writes=[Bt])
        ncb = 512 // W1
        nsl = 512 // (2 * nk2)
        for g in range(4):
            P.dma("sync", fT1[0:T1, :, :], d["f_all_" + name][g].rearrange("(a b) c -> a b c", b=128), writes=[BfT1])
            ev = 0
            for cb in range(128 // ncb):
                pb = cb % 2
                for ci in range(ncb):
                    c = cb * ncb + ci
                    P.op("tensor", lambda e, c=c, ci=ci, pb=pb: e.matmul(pG[pb][:, ci * W1:(ci + 1) * W1], lhsT=fT1[0:T1, :, c], rhs=cs1[0:T1, :], start=True, stop=True),
                         [BfT1, Bt], [BpG[pb]])
                src = pG[pb][:, 0:ncb * W1].rearrange("p (c w) -> p c w", w=W1)
                dst = Gs[:, cb * ncb:(cb + 1) * ncb, 0:W1]
                if ev % 2 == 0:
                    P.op("scalar", lambda e, src=src, dst=dst: e.activation(out=dst, in_=src, func=AF.Copy), [BpG[pb]], [BGs])
                else:
                    P.op("vector", lambda e, src=src, dst=dst: e.tensor_copy(out=dst, in_=src), [BpG[pb]], [BGs])
                ev += 1
            XT4 = XTs[:, :, 0:ntok].rearrange("p r (k2 k1) -> p r k2 k1", k1=T1)
            for sb_ in range(T1 // nsl):
                pb = sb_ % 2
                for si in range(nsl):
                    k1 = sb_ * nsl + si
                    o = pX[pb][:, si * 2 * nk2:(si + 1) * 2 * nk2]
                    P.op("tensor", lambda e, o=o, k1=k1: e.matmul(o, lhsT=Gs[:, :, k1], rhs=tab[:, k1, 0, :], start=True, stop=False), [BGs, Bt], [BpX[pb]])
                    P.op("tensor", lambda e, o=o, k1=k1: e.matmul(o, lhsT=Gs[:, :, T1 + k1], rhs=tab[:, k1, 1, :], start=False, stop=True), [BGs, Bt], [BpX[pb]])
                src = pX[pb][:].rearrange("p (s r k) -> p s r k", r=2, k=nk2)
                dst = XT4[:, :, :, sb_ * nsl:(sb_ + 1) * nsl].rearrange("p r k2 k1 -> p k1 r k2")
                if sb_ % 2 == 0:
                    P.op("scalar", lambda e, src=src, dst=dst: e.activation(out=dst, in_=src, func=AF.Copy), [BpX[pb]], [BXTs])
                else:
                    P.op("vector", lambda e, src=src, dst=dst: e.tensor_copy(out=dst, in_=src), [BpX[pb]], [BXTs])
            P.dma("sync", d["XT_" + name][:, 2 * g:2 * g + 2, :], XTs[:, :, 0:ntok], reads=[BXTs], writes=[C.B_XT])
        P.release(m2)
    P.release(m)


def load_w(P, tile, src, nk, Bw, width=None):
    for k in range(nk):
        P.dma("gpsimd", tile[:, k, :], src[k * 128:(k + 1) * 128, :], writes=[Bw])


def stage_mix(P, C, d, segs):
    m = P.mark()
    Bw = Buf()
    wfm = P.sb("m_wfm", [128, 8, FM_W], BF16); load_w(P, wfm, d["w_fm"], 8, Bw)
    wba = P.sb("m_wba", [128, 8, D], BF16); load_w(P, wba, d["w_ba"], 8, Bw)
    wbs = P.sb("m_wbs", [128, 4, D], BF16); load_w(P, wbs, d["w_bs"], 4, Bw)
    wbf0 = P.sb("m_wbf0", [128, 4, D], BF16); load_w(P, wbf0, d["w_bf"], 4, Bw)
    wbf = P.sb("m_wbf", [128, 8, D], BF16); Bwbf = Buf()
    wo = P.sb("m_wo", [128, 8, D], BF16); load_w(P, wo, d["w_o"], 8, Bw)
    ccs = P.sb("m_ccs", [128, 2, 128], BF16)
    P.dma("gpsimd", ccs[:], d["ccs"], writes=[Bw])
    lng = P.sb("m_lng", [128, 8], F32); lnb = P.sb("m_lnb", [128, 8], F32)
    P.dma("sync", lng[:], d["ln1_gT"], writes=[Bw]); P.dma("sync", lnb[:], d["ln1_bT"], writes=[Bw])
    mod = P.sb("m_mod", [128, 48, 2], F32)
    P.dma("sync", mod[:], d["modT"], writes=[Bw])
    pg = [P.ps(f"m_pg{i}", [128, 512]) for i in range(3)]; Bpg = [Buf() for _ in range(3)]
    pb = [P.ps(f"m_pb{i}", [128, 512]) for i in range(3)]; Bpb = [Buf() for _ in range(3)]
    C.ps_a = P.ps("m_psa", [128, 512]); C.ps_b = P.ps("m_psb", [128, 512]); C.B_psa, C.B_psb = Buf(), Buf()
    ln_alloc(P, C, 256)
    for g in range(4):
        for r in range(2):
            for n2 in range(2):
                P.op("tensor", lambda e, g=g, r=r, n2=n2: e.matmul(pg[n2][:], lhsT=ccs[:, r, :], rhs=wbf0[:, g, n2 * 512:(n2 + 1) * 512], start=True, stop=True),
                     [Bw], [Bpg[n2]])
                P.op("vector", lambda e, g=g, r=r, n2=n2: e.tensor_copy(out=wbf[:, 2 * g + r, n2 * 512:(n2 + 1) * 512], in_=pg[n2][:]), [Bpg[n2]], [Bwbf])
    hT = P.sb("m_hT", [128, 8, 256], BF16); BhT = Buf()
    AT = P.sb("m_AT", [128, 8, 256], BF16); BAT = Buf()
    XT = P.sb("m_XT", [128, 8, 256], BF16); BXT = Buf()
    smT = P.sb("m_smT", [128, 4, 256], BF16); BsmT = Buf()
    xs = P.sb("m_xs", [128, 8, 256], F32); Bxs = Buf()
    gT = P.sb("m_gT", [128, 4, 256], BF16); BgT = Buf()
    ut = P.sb("m_ut", [128, 256], F32); But = Buf()
    sgs = [P.sb(f"m_sg{i}", [128, 256], F32) for i in range(3)]; Bsg = [Buf() for _ in range(3)]
    macc = P.sb("m_macc", [128, 256], F32); Bmacc = Buf()
    mtmp = P.sb("m_mtmp", [128, 256], F32); Bmtmp = Buf()
    mg = P.sb("m_mg", [128, 8, 256], BF16); Bmg = Buf()
    y = P.sb("m_y", [128, 8, 256], F32); By = Buf()
    x1 = P.sb("m_x1", [128, 8, 256], F32); Bx1 = Buf()
    for (name, ntok, nb, col, rope) in segs:
        nb = 256
        for b in range(ntok // nb):
            t0 = b * nb
            ts_ = slice(t0, t0 + nb)
            P.dma("sync", hT[:, :, 0:nb], d["hT_" + name][:, :, ts_], writes=[BhT])
            P.dma("sync", AT[:, :, 0:nb], d["AT_" + name][:, :, ts_], reads=[C.B_AT], writes=[BAT])
            P.dma("sync", XT[:, :, 0:nb], d["XT_" + name][:, :, ts_], reads=[C.B_XT], writes=[BXT])
            P.dma("sync", smT[:, :, 0:nb], d["smT_" + name][:, :, ts_], writes=[BsmT])
            P.dma("sync", xs[:, :, 0:nb], d["xT_" + name][:, :, ts_], writes=[Bxs])
            for j in range(4):
                for k in range(8):
                    P.op("tensor", lambda e, j=j, k=k: e.matmul(pg[0][:, 0:nb], lhsT=wfm[:, k, j * 128:(j + 1) * 128], rhs=hT[:, k, 0:nb], start=(k == 0), stop=(k == 7)),
                         [Bw, BhT], [Bpg[0]])
                P.op("scalar", lambda e: e.activation(out=ut[:, 0:nb], in_=pg[0][:, 0:nb], func=AF.Gelu_apprx_tanh), [Bpg[0]], [But])
                P.op("vector", lambda e, j=j: e.tensor_tensor(out=gT[:, j, 0:nb], in0=ut[:, 0:nb], in1=smT[:, j, 0:nb], op=ALU.mult), [But, BsmT], [BgT])
            for oc in range(8):
                ocs = slice(oc * 128, (oc + 1) * 128)
                for br in range(3):
                    c0 = 512 + br * 1024 + oc * 128
                    for k in range(8):
                        P.op("tensor", lambda e, br=br, k=k, c0=c0: e.matmul(pg[br][:, 0:nb], lhsT=wfm[:, k, c0:c0 + 128], rhs=hT[:, k, 0:nb], start=(k == 0), stop=(k == 7)),
                             [Bw, BhT], [Bpg[br]])
                    P.op("scalar", lambda e, br=br: e.activation(out=sgs[br][:, 0:nb], in_=pg[br][:, 0:nb], func=AF.Sigmoid), [Bpg[br]], [Bsg[br]])
                for k in range(8):
                    P.op("tensor", lambda e, k=k, ocs=ocs: e.matmul(pb[0][:, 0:nb], lhsT=wba[:, k, ocs], rhs=AT[:, k, 0:nb], start=(k == 0), stop=(k == 7)), [Bw, BAT], [Bpb[0]])
                for k in range(4):
                    P.op("tensor", lambda e, k=k, ocs=ocs: e.matmul(pb[1][:, 0:nb], lhsT=wbs[:, k, ocs], rhs=gT[:, k, 0:nb], start=(k == 0), stop=(k == 3)), [Bw, BgT], [Bpb[1]])
                for k in range(8):
                    P.op("tensor", lambda e, k=k, ocs=ocs: e.matmul(pb[2][:, 0:nb], lhsT=wbf[:, k, ocs], rhs=XT[:, k, 0:nb], start=(k == 0), stop=(k == 7)), [Bwbf, BXT], [Bpb[2]])
                P.op("vector", lambda e: e.tensor_tensor(out=macc[:, 0:nb], in0=pb[0][:, 0:nb], in1=sgs[0][:, 0:nb], op=ALU.mult), [Bpb[0], Bsg[0]], [Bmacc])
                P.op("vector", lambda e: e.tensor_tensor(out=mtmp[:, 0:nb], in0=pb[1][:, 0:nb], in1=sgs[1][:, 0:nb], op=ALU.mult), [Bpb[1], Bsg[1]], [Bmtmp])
                P.op("vector", lambda e: e.tensor_tensor(out=macc[:, 0:nb], in0=macc[:, 0:nb], in1=mtmp[:, 0:nb], op=ALU.add), [Bmacc, Bmtmp], [Bmacc])
                P.op("vector", lambda e: e.tensor_tensor(out=mtmp[:, 0:nb], in0=pb[2][:, 0:nb], in1=sgs[2][:, 0:nb], op=ALU.mult), [Bpb[2], Bsg[2]], [Bmtmp])
                P.op("vector", lambda e, oc=oc: e.tensor_tensor(out=mg[:, oc, 0:nb], in0=macc[:, 0:nb], in1=mtmp[:, 0:nb], op=ALU.add), [Bmacc, Bmtmp], [Bmg])
            for oc in range(8):
                ocs = slice(oc * 128, (oc + 1) * 128)
                for k in range(8):
                    P.op("tensor", lambda e, k=k, ocs=ocs: e.matmul(pb[0][:, 0:nb], lhsT=wo[:, k, ocs], rhs=mg[:, k, 0:nb], start=(k == 0), stop=(k == 7)), [Bw, Bmg], [Bpb[0]])
                P.op("vector", lambda e, oc=oc: e.tensor_scalar(out=mtmp[:, 0:nb], in0=pb[0][:, 0:nb], scalar1=mod[:, 16 + oc, col:col + 1], scalar2=None, op0=ALU.mult),
                     [Bpb[0], Bw], [Bmtmp])
                P.op("vector", lambda e, oc=oc: e.scalar_tensor_tensor(out=y[:, oc, 0:nb], in0=xs[:, oc, 0:nb], scalar=ALPHA, in1=mtmp[:, 0:nb], op0=ALU.mult, op1=ALU.add),
                     [Bxs, Bmtmp], [By])
            ln_affine(P, C, y, By, nb, lng, Bw, lnb, x1, Bx1)
            P.dma("sync", d["x1T_" + name][:, :, ts_], x1[:, :, 0:nb], reads=[Bx1], writes=[C.B_x1T])
    P.release(m)


def stage_ffn(P, C, d, segs):
    m = P.mark()
    Bw = Buf()
    wup = P.sb("n_wup", [128, 8, 2 * FFN], BF16); load_w(P, wup, d["w_up"], 8, Bw)
    wdn = P.sb("n_wdn", [128, 22, D], BF16); load_w(P, wdn, d["w_dn"], 22, Bw)
    lng = P.sb("n_lng", [128, 8], F32); lnb = P.sb("n_lnb", [128, 8], F32)
    P.dma("sync", lng[:], d["ln2_gT"], writes=[Bw]); P.dma("sync", lnb[:], d["ln2_bT"], writes=[Bw])
    mod = P.sb("n_mod", [128, 48, 2], F32)
    mod1 = P.sb("n_mod1", [128, 8, 2], F32)
    P.dma("sync", mod[:], d["modT"], writes=[Bw])
    P.op("vector", lambda e: e.tensor_scalar(out=mod1[:], in0=mod[:, 32:40, :], scalar1=1.0, scalar2=None, op0=ALU.add), [Bw], [Bw])
    pg = [P.ps(f"n_pg{i}", [128, 512]) for i in range(2)]; Bpg = [Buf() for _ in range(2)]
    pu = [P.ps(f"n_pu{i}", [128, 512]) for i in range(2)]; Bpu = [Buf() for _ in range(2)]
    pd = [P.ps(f"n_pd{i}", [128, 512]) for i in range(2)]; Bpd = [Buf() for _ in range(2)]
    C.ps_a = P.ps("n_psa", [128, 512]); C.ps_b = P.ps("n_psb", [128, 512]); C.B_psa, C.B_psb = Buf(), Buf()
    ln_alloc(P, C, 256)
    x1 = P.sb("n_x1", [128, 8, 256], F32); Bx1 = Buf()
    h2 = P.sb("n_h2", [128, 8, 256], BF16); Bh2 = Buf()
    sgt = [P.sb(f"n_sgt{i}", [128, 256], F32) for i in range(2)]; Bsgt = [Buf(), Buf()]
    hid = P.sb("n_hid", [128, 22, 256], BF16); Bhid = Buf()
    mtmp = P.sb("n_mtmp", [128, 256], F32); Bmtmp = Buf()
    y = P.sb("n_y", [128, 8, 256], F32); By = Buf()
    x2 = P.sb("n_x2", [128, 8, 256], F32); Bx2 = Buf()
    for (name, ntok, nb, col, rope) in segs:
        nb = 256
        for b in range(ntok // nb):
            ts_ = slice(b * nb, (b + 1) * nb)
            P.dma("sync", x1[:, :, 0:nb], d["x1T_" + name][:, :, ts_], reads=[C.B_x1T], writes=[Bx1])
            ln_affine(P, C, x1, Bx1, nb, mod1[:, :, col], Bw, mod[:, 24:32, col], h2, Bh2)
            for j in range(22):
                i2 = j % 2
                for k in range(8):
                    P.op("tensor", lambda e, j=j, k=k, i2=i2: e.matmul(pg[i2][:, 0:nb], lhsT=wup[:, k, j * 128:(j + 1) * 128], rhs=h2[:, k, 0:nb], start=(k == 0), stop=(k == 7)),
                         [Bw, Bh2], [Bpg[i2]])
                for k in range(8):
                    P.op("tensor", lambda e, j=j, k=k, i2=i2: e.matmul(pu[i2][:, 0:nb], lhsT=wup[:, k, FFN + j * 128:FFN + (j + 1) * 128], rhs=h2[:, k, 0:nb], start=(k == 0), stop=(k == 7)),
                         [Bw, Bh2], [Bpu[i2]])
                P.op("scalar", lambda e, i2=i2: e.activation(out=sgt[i2][:, 0:nb], in_=pg[i2][:, 0:nb], func=AF.Silu), [Bpg[i2]], [Bsgt[i2]])
                P.op("vector", lambda e, j=j, i2=i2: e.tensor_tensor(out=hid[:, j, 0:nb], in0=pu[i2][:, 0:nb], in1=sgt[i2][:, 0:nb], op=ALU.mult), [Bpu[i2], Bsgt[i2]], [Bhid])
            for oc in range(8):
                i2 = oc % 2
                ocs = slice(oc * 128, (oc + 1) * 128)
                for k in range(22):
                    P.op("tensor", lambda e, k=k, ocs=ocs, i2=i2: e.matmul(pd[i2][:, 0:nb], lhsT=wdn[:, k, ocs], rhs=hid[:, k, 0:nb], start=(k == 0), stop=(k == 21)), [Bw, Bhid], [Bpd[i2]])
                P.op("vector", lambda e, oc=oc, i2=i2: e.tensor_scalar(out=mtmp[:, 0:nb], in0=pd[i2][:, 0:nb], scalar1=mod[:, 40 + oc, col:col + 1], scalar2=None, op0=ALU.mult),
                     [Bpd[i2], Bw], [Bmtmp])
                P.op("vector", lambda e, oc=oc: e.scalar_tensor_tensor(out=y[:, oc, 0:nb], in0=x1[:, oc, 0:nb], scalar=ALPHA, in1=mtmp[:, 0:nb], op0=ALU.mult, op1=ALU.add),
                     [Bx1, Bmtmp], [By])
            ln_affine(P, C, y, By, nb, lng, Bw, lnb, x2, Bx2)
            P.dma("sync", d["xo_" + name][:, :, ts_], x2[:, :, 0:nb], reads=[Bx2])
    P.release(m)


def build_post(segs, stages=("attn", "four", "mix", "ffn")):
    nc = bass.Bass("TRN2", target_bir_lowering=False)
    P = Prog(nc)
    C = Ctx()
    d = {}
    I_, O_ = "ExternalInput", "ExternalOutput"
    d["ident"] = dram(nc, "ident", [128, 128], F32, I_)
    d["ones"] = dram(nc, "ones", [2, 128, 128], F32, I_)
    d["kT_all"] = dram(nc, "kT_all", [128, 2, SEQ + CTX], BF16, I_)
    d["v_all"] = dram(nc, "v_all", [2, 128, NKT, 128], BF16, I_)
    d["modT"] = dram(nc, "modT", [128, 48, 2], F32, I_)
    d["w_fm"] = dram(nc, "w_fm", [D, FM_W], F32, I_)
    d["w_ba"] = dram(nc, "w_ba", [D, D], F32, I_)
    d["w_bs"] = dram(nc, "w_bs", [512, D], F32, I_)
    d["w_bf"] = dram(nc, "w_bf", [512, D], F32, I_)
    d["w_o"] = dram(nc, "w_o", [D, D], F32, I_)
    d["w_up"] = dram(nc, "w_up", [D, 2 * FFN], F32, I_)
    d["w_dn"] = dram(nc, "w_dn", [FFN, D], F32, I_)
    d["ccs"] = dram(nc, "ccs", [128, 2, 128], F32, I_)
    for n in ("ln1_gT", "ln1_bT", "ln2_gT", "ln2_bT"):
        d[n] = dram(nc, n, [128, 8], F32, I_)
    for (name, ntok, nb, col, rope) in segs:
        T1, nk2 = (128, 16) if name == "x" else (2, 128)
        tot = SEQ if name == "x" else CTX
        d["cs1_" + name] = dram(nc, "cs1_" + name, [T1, 2 * T1], F32, I_)
        d["tab_" + name] = dram(nc, "tab_" + name, [128, T1, 2, 2 * nk2], F32, I_)
        d["f_all_" + name] = dram(nc, "f_all_" + name, [4, tot, 128], BF16, I_)
        for n in ("qT_", "hT_"):
            d[n + name] = dram(nc, n + name, [128, 8, ntok], BF16, I_)
        d["smT_" + name] = dram(nc, "smT_" + name, [128, 4, ntok], BF16, I_)
        d["xT_" + name] = dram(nc, "xT_" + name, [128, 8, ntok], F32, I_)
        d["AT_" + name] = dram(nc, "AT_" + name, [128, 8, ntok], BF16, O_)
        d["XT_" + name] = dram(nc, "XT_" + name, [128, 8, ntok], BF16, O_)
        d["x1T_" + name] = dram(nc, "x1T_" + name, [128, 8, ntok], F32, O_)
        d["xo_" + name] = dram(nc, "xo_" + name, [128, 8, ntok], F32, O_)
    C.B_AT, C.B_XT, C.B_x1T = Buf(), Buf(), Buf()
    consts_alloc(P, C, d["ident"], d["ones"])
    if "attn" in stages:
        stage_attn(P, C, d, segs)
    if "four" in stages:
        stage_four(P, C, d, segs)
    if "mix" in stages:
        stage_mix(P, C, d, segs)
    if "ffn" in stages:
        stage_ffn(P, C, d, segs)
    P.finish()
    return nc


def dft_tables(name, core):
    if name == "x":
        T, T1, nk2, k2_0 = SEQ, 128, 16, core * 16
    else:
        T, T1, nk2, k2_0 = CTX, 2, 128, 0
    t1 = np.arange(T1)[:, None].astype(np.float64); k1 = np.arange(T1)[None, :].astype(np.float64)
    a1 = 2 * np.pi * t1 * k1 / T1
    cs1 = np.concatenate([np.cos(a1), -np.sin(a1)], axis=1).astype(np.float32)
    t2 = np.arange(128)[:, None, None].astype(np.float64)
    k = (np.arange(T1)[None, :, None] + T1 * (k2_0 + np.arange(nk2))[None, None, :]).astype(np.float64)
    ang = 2 * np.pi * ((t2 * k) % T) / T
    cc, ss = np.cos(ang) / np.sqrt(T), np.sin(ang) / np.sqrt(T)
    tab = np.stack([np.concatenate([cc, -ss], axis=-1), np.concatenate([ss, cc], axis=-1)], axis=2)
    return cs1, np.ascontiguousarray(tab.astype(np.float32))


def chan_tables():
    c = np.arange(128)[:, None].astype(np.float64); c2 = np.arange(128)[None, :].astype(np.float64)
    a = 2 * np.pi * c * c2 / 128
    return np.ascontiguousarray(np.stack([np.cos(a), np.sin(a)], axis=1).astype(np.float32) / np.float32(np.sqrt(128.0)))


def post_inputs(l, inp, pre, xT_cores, ctxT, segs):
    f32 = np.float32
    w_in = inp["w_in"][l]
    kT_all = np.concatenate([np.asarray(pre[i]["kT_x"]) for i in range(NCORES)] + [np.asarray(pre[0]["kT_c"])], axis=2)
    v_cat = np.concatenate([np.asarray(pre[i]["v_x"]) for i in range(NCORES)] + [np.asarray(pre[0]["v_c"])], axis=1)
    v_all = np.ascontiguousarray(v_cat.reshape(128, NKT, 2, 128).transpose(2, 0, 1, 3))
    f_all_x = np.ascontiguousarray(np.concatenate([np.asarray(pre[i]["f_x"]) for i in range(NCORES)], axis=1))
    common = {
        "ident": np.eye(128, dtype=f32),
        "ones": np.stack([np.full((128, 128), 1.0 / 1024, f32), np.ones((128, 128), f32)]),
        "kT_all": np.ascontiguousarray(kT_all), "v_all": v_all,
        "modT": np.asarray(pre[0]["modT"]),
        "w_fm": np.ascontiguousarray(np.concatenate([w_in[:, 1536:2048], w_in[:, 3072:6144]], axis=1)),
        "w_ba": np.ascontiguousarray(inp["w_br_attn"][l]), "w_bs": np.ascontiguousarray(inp["w_br_sgu"][l]),
        "w_bf": np.ascontiguousarray(inp["w_br_fourier"][l]), "w_o": np.ascontiguousarray(inp["w_out"][l]),
        "w_up": np.ascontiguousarray(inp["w_up"][l]), "w_dn": np.ascontiguousarray(inp["w_down"][l]),
        "ccs": chan_tables(),
        "ln1_gT": vecT(inp["ln1_g"][l], 8), "ln1_bT": vecT(inp["ln1_b"][l], 8),
        "ln2_gT": vecT(inp["ln2_g"][l], 8), "ln2_bT": vecT(inp["ln2_b"][l], 8),
        "f_all_x": f_all_x,
    }
    names = [s[0] for s in segs]
    if "c" in names:
        cs1, tab = dft_tables("c", 0)
        common.update({"cs1_c": cs1, "tab_c": tab, "f_all_c": np.asarray(pre[0]["f_c"]), "xT_c": ctxT})
    maps = []
    for i in range(NCORES):
        m = dict(common)
        cs1, tab = dft_tables("x", i)
        m["cs1_x"], m["tab_x"] = cs1, tab
        m["xT_x"] = xT_cores[i]
        for n in ("qT_", "hT_", "smT_"):
            for sname in names:
                m[n + sname] = np.asarray(pre[i][n + sname])
        maps.append(m)
    return maps


def run_post(l, inp, pre, xT_cores, ctxT, with_ctx, stages=("attn", "four", "mix", "ffn")):
    segs = [SEG_C, SEG_X] if with_ctx else [SEG_X]
    nc = get_prog(("post", with_ctx, stages), lambda: build_post(segs, stages))
    res = run_bass_kernel_spmd(nc, post_inputs(l, inp, pre, xT_cores, ctxT, segs), core_ids=list(range(NCORES)))
    return res.results


def kernel(**inp):
    inp = {k: np.asarray(v) for k, v in inp.items()}
    x = inp["x"][0]
    xT_cores = [fm(x[i * TOK:(i + 1) * TOK]) for i in range(NCORES)]
    ctxT = fm(inp["ctx"][0])
    for l in range(DEPTH):
        last = l == DEPTH - 1
        pre = run_pre(l, inp, xT_cores, ctxT)
        post = run_post(l, inp, pre, xT_cores, ctxT, with_ctx=not last)
        xT_cores = [np.asarray(post[i]["xo_x"]) for i in range(NCORES)]
        if not last:
            ctxT = np.asarray(post[0]["xo_c"])
    out = np.concatenate([unfm(a) for a in xT_cores], axis=0)[None]
    return out.astype(np.float32)
```
